# Optimizing a Trainium2 kernel written in Bass

```python
import math
import jax, jax.numpy as jnp
from jax import lax
import numpy as np

D_MODEL = 1024
BATCH = 16
SEQ = 2048
DEPTH = 2

CHUNK = 64
Q_BLOCK = 128
D_RNN = 1024
N_RNN_BLOCKS = 8
RNN_BLOCK = D_RNN // N_RNN_BLOCKS
CONV_WIDTH = 4
LRU_C = 8.0
N_HEADS = 16
HEAD_DIM = 64
D_ATTN = N_HEADS * HEAD_DIM
D_FF = 2816
EPS = 1e-6
N_BRANCH = 2
OFF_LRU_X = 0
OFF_LRU_G = OFF_LRU_X + D_RNN
OFF_Q = OFF_LRU_G + D_RNN
OFF_K = OFF_Q + D_ATTN
OFF_V = OFF_K + D_ATTN
OFF_F = OFF_V + D_ATTN
OFF_GATE = OFF_F + N_HEADS
D_IN = OFF_GATE + N_BRANCH * D_MODEL

kernel_name = "hybrid_rglru_fox_macaron_encoder"


def rms_norm(x, g):
    xf = x.astype(jnp.float32)
    y = xf * lax.rsqrt(jnp.mean(xf * xf, axis=-1, keepdims=True) + EPS)
    return (y * g.astype(jnp.float32)).astype(x.dtype)


def swiglu_ffn(h, w_up, w_down):
    gate, up = jnp.split(h @ w_up, 2, axis=-1)
    return (jax.nn.silu(gate) * up) @ w_down


def causal_depthwise_conv(x, w, b):
    y = lax.conv_general_dilated(
        x, w[:, None, :].astype(x.dtype), window_strides=(1,),
        padding=[(CONV_WIDTH - 1, 0)],
        dimension_numbers=("NWC", "WIO", "NWC"),
        feature_group_count=x.shape[-1])
    return y + b


def rg_lru(x, w_a, b_a, w_x, b_x, lam):
    bsz, seq, _ = x.shape
    xb = x.reshape(bsz, seq, N_RNN_BLOCKS, RNN_BLOCK)
    r = jax.nn.sigmoid((jnp.einsum("bsnc,ncd->bsnd", xb, w_a).reshape(bsz, seq, D_RNN) + b_a).astype(jnp.float32))
    i = jax.nn.sigmoid((jnp.einsum("bsnc,ncd->bsnd", xb, w_x).reshape(bsz, seq, D_RNN) + b_x).astype(jnp.float32))
    log_a = -LRU_C * r * jax.nn.softplus(-lam.astype(jnp.float32))
    a = jnp.exp(log_a)
    u = jnp.sqrt(-jnp.expm1(2.0 * log_a)) * (i * x.astype(jnp.float32))

    def combine(left, right):
        a_l, b_l = left
        a_r, b_r = right
        return a_l * a_r, a_r * b_l + b_r

    _, h = lax.associative_scan(combine, (a, u), axis=1)
    return h.astype(x.dtype)


def forgetting_attention(q, k, v, f_logit, g_q, g_k):
    bsz, seq = q.shape[0], q.shape[1]
    qn = rms_norm(q, g_q).transpose(0, 2, 1, 3)
    kn = rms_norm(k, g_k).transpose(0, 2, 1, 3)
    vh = v.transpose(0, 2, 1, 3)
    cum = jnp.cumsum(jax.nn.log_sigmoid(f_logit.astype(jnp.float32)), axis=1).transpose(0, 2, 1)
    scale = HEAD_DIM ** -0.5
    outs = []
    for blk in range(seq // Q_BLOCK):
        q0 = blk * Q_BLOCK
        q1 = q0 + Q_BLOCK
        s = jnp.einsum("bhqd,bhkd->bhqk", qn[:, :, q0:q1], kn[:, :, :q1]).astype(jnp.float32) * scale
        s = s + cum[:, :, q0:q1, None] - cum[:, :, None, :q1]
        mask = jnp.arange(q0, q1)[:, None] >= jnp.arange(q1)[None, :]
        s = jnp.where(mask, s, -jnp.inf)
        p = jax.nn.softmax(s, axis=-1)
        outs.append(jnp.einsum("bhqk,bhkd->bhqd", p.astype(vh.dtype), vh[:, :, :q1]))
    o = jnp.concatenate(outs, axis=2)
    return o.transpose(0, 2, 1, 3).reshape(bsz, seq, D_ATTN)


def setup_inputs(seed: int = 0) -> dict:
    key = jax.random.key(seed)
    ks = jax.random.split(key, 24)
    L, D, F = DEPTH, D_MODEL, D_FF
    nrm = lambda k, shape, fan_in: jax.random.normal(k, shape, jnp.float32) * (fan_in ** -0.5)
    gain = lambda k, shape: 1.0 + 0.02 * jax.random.normal(k, shape, jnp.float32)
    small = lambda k, shape: 0.01 * jax.random.normal(k, shape, jnp.float32)
    u = jax.random.uniform(ks[13], (L, D_RNN), jnp.float32, 0.9, 0.999)
    a0 = u ** (1.0 / LRU_C)
    lam = jnp.log(a0) - jnp.log1p(-a0)
    return {
        "x": jax.random.normal(ks[0], (BATCH, SEQ, D), jnp.float32),
        "g_ffn1": gain(ks[1], (L, D)),
        "w_up1": nrm(ks[2], (L, D, 2 * F), D),
        "w_down1": nrm(ks[3], (L, F, D), F),
        "g_mix": gain(ks[4], (L, D)),
        "w_in": nrm(ks[5], (L, D, D_IN), D),
        "b_gate": small(ks[6], (L, N_BRANCH * D)),
        "conv_w": nrm(ks[7], (L, CONV_WIDTH, D_RNN), CONV_WIDTH),
        "conv_b": small(ks[8], (L, D_RNN)),
        "w_a": nrm(ks[9], (L, N_RNN_BLOCKS, RNN_BLOCK, RNN_BLOCK), RNN_BLOCK),
        "b_a": small(ks[10], (L, D_RNN)),
        "w_x": nrm(ks[11], (L, N_RNN_BLOCKS, RNN_BLOCK, RNN_BLOCK), RNN_BLOCK),
        "b_x": small(ks[12], (L, D_RNN)),
        "lam": lam,
        "g_q": gain(ks[14], (L, HEAD_DIM)),
        "g_k": gain(ks[15], (L, HEAD_DIM)),
        "b_forget": jax.random.uniform(ks[16], (L, N_HEADS), jnp.float32, 1.0, 6.0),
        "w_lru_out": nrm(ks[17], (L, D_RNN, D), D_RNN),
        "w_attn_out": nrm(ks[18], (L, D_ATTN, D), D_ATTN),
        "w_o": nrm(ks[19], (L, D, D), D),
        "g_ffn2": gain(ks[20], (L, D)),
        "w_up2": nrm(ks[21], (L, D, 2 * F), D),
        "w_down2": nrm(ks[22], (L, F, D), F),
    }


def reference(x, g_ffn1, w_up1, w_down1, g_mix, w_in, b_gate, conv_w, conv_b,
              w_a, b_a, w_x, b_x, lam, g_q, g_k, b_forget, w_lru_out, w_attn_out,
              w_o, g_ffn2, w_up2, w_down2):
    bsz, seq, _ = x.shape
    assert seq % CHUNK == 0 and seq % Q_BLOCK == 0
    for l in range(DEPTH):
        x = x + 0.5 * swiglu_ffn(rms_norm(x, g_ffn1[l]), w_up1[l], w_down1[l])

        h = rms_norm(x, g_mix[l])
        z = h @ w_in[l]
        z_lx = z[..., OFF_LRU_X:OFF_LRU_G]
        z_lg = z[..., OFF_LRU_G:OFF_Q]
        q = z[..., OFF_Q:OFF_K].reshape(bsz, seq, N_HEADS, HEAD_DIM)
        k = z[..., OFF_K:OFF_V].reshape(bsz, seq, N_HEADS, HEAD_DIM)
        v = z[..., OFF_V:OFF_F].reshape(bsz, seq, N_HEADS, HEAD_DIM)
        f_logit = z[..., OFF_F:OFF_GATE] + b_forget[l]
        gates = jax.nn.sigmoid(z[..., OFF_GATE:] + b_gate[l])
        gate_lru, gate_attn = jnp.split(gates, N_BRANCH, axis=-1)

        xr = causal_depthwise_conv(z_lx, conv_w[l], conv_b[l])
        hr = rg_lru(xr, w_a[l], b_a[l], w_x[l], b_x[l], lam[l])
        y_lru = (jax.nn.gelu(z_lg) * hr) @ w_lru_out[l]

        o = forgetting_attention(q, k, v, f_logit, g_q[l], g_k[l])
        y_attn = o @ w_attn_out[l]

        m = gate_lru * y_lru + gate_attn * y_attn
        x = x + m @ w_o[l]

        x = x + 0.5 * swiglu_ffn(rms_norm(x, g_ffn2[l]), w_up2[l], w_down2[l])
    return x
```

```python
import numpy as np
from contextlib import ExitStack
import concourse.bass as bass
import concourse.mybir as mybir
from concourse.bass_utils import run_bass_kernel_spmd

F32 = mybir.dt.float32
BF16 = mybir.dt.bfloat16
AF = mybir.ActivationFunctionType
ALU = mybir.AluOpType

D = 1024
SEQ = 2048
NL = 2
FF = 2816
KC = 8
JC = 22
TB = 512
NTB = SEQ // TB
NH = 16
HD = 64
EPS = 1e-6
OFF_LX, OFF_LG, OFF_Q, OFF_K, OFF_V, OFF_F, OFF_GATE = 0, 1024, 2048, 3072, 4096, 5120, 5136
NEG = -30000.0

PP_GF1, PP_GMIX, PP_GF2 = 0, 8, 16
PP_CW, PP_CB, PP_BA, PP_BX, PP_LAM = 24, 56, 64, 72, 80
PP_BG, PP_GQ, PP_GK, PP_BF = 88, 104, 105, 106
NPP = 107
DV_COEF, DV_HCOEF, DV_GQS, DV_NBF, DV_HBA, DV_HBX = 0, 8, 16, 17, 18, 26
NDV = 34


class Buf:
    __slots__ = ("name", "writer", "readers", "excl", "dsem", "dcount")

    def __init__(self, name, excl=False):
        self.name = name
        self.writer = None
        self.readers = []
        self.excl = excl
        self.dsem = None
        self.dcount = 0


class Sync:
    def __init__(self, nc, stack):
        self.nc = nc
        self.stack = stack
        self.eng = {"pe": nc.tensor, "act": nc.scalar, "dve": nc.vector, "pool": nc.gpsimd, "sp": nc.sync}
        self.sem = {}
        self.cnt = {}
        self.seen = {}
        for e in self.eng:
            self.sem[e] = stack.enter_context(nc.semaphore("sem_" + e))
            self.cnt[e] = 0
            self.seen[e] = {}
        self.ndsem = 0
        self.dry = False
        self.ninst = 0

    def _need(self, e, deps):
        seen = self.seen[e]
        best = {}
        for d in deps:
            if d is None:
                continue
            k, v = d
            if k == e and e == "pe":
                continue
            if seen.get(k, 0) >= v:
                continue
            if best.get(k, 0) < v:
                best[k] = v
        for k, v in best.items():
            semobj = self.sem[k] if isinstance(k, str) else k.dsem
            self.eng[e].wait_ge(semobj, v)
            seen[k] = v

    @staticmethod
    def _deps_for(e, reads, writes):
        deps = []
        for b in reads:
            deps.append(b.writer)
            if b.excl:
                for r in b.readers:
                    if r[0] != e:
                        deps.append(r)
        for b in writes:
            deps.append(b.writer)
            deps.extend(b.readers)
        return deps

    @staticmethod
    def _commit(tok, reads, writes):
        for b in reads:
            b.readers = [r for r in b.readers if r[0] != tok[0]]
            b.readers.append(tok)
        for b in writes:
            b.writer = tok
            b.readers = []

    def op(self, e, fn, reads=(), writes=(), inc=True):
        if self.dry:
            return None
        self._need(e, self._deps_for(e, reads, writes))
        inst = fn()
        self.ninst += 1
        if inc:
            self.cnt[e] += 1
            inst.then_inc(self.sem[e], 1)
            tok = (e, self.cnt[e])
        else:
            tok = (e, self.cnt[e] + 1)
        self._commit(tok, reads, writes)
        return inst

    def dma(self, q, out_ap, in_ap, reads=(), writes=(), dbuf=None):
        if self.dry:
            return None
        if dbuf is None:
            dbuf = writes[0] if writes else reads[0]
        if dbuf.dsem is None:
            dbuf.dsem = self.stack.enter_context(self.nc.semaphore("dsem_%d" % self.ndsem))
            self.ndsem += 1
        self._need(q, [d for d in self._deps_for(q, reads, writes) if d is not None and d[0] is not dbuf])
        inst = self.eng[q].dma_start(out=out_ap, in_=in_ap)
        self.ninst += 1
        dbuf.dcount += 16
        inst.then_inc(dbuf.dsem, 16)
        tok = (dbuf, dbuf.dcount)
        self._commit(tok, reads, writes)
        return inst

    def barrier(self, engines=("pe", "act", "dve", "pool")):
        if self.dry:
            return
        for e in engines:
            self._need(e, [(o, self.cnt[o]) for o in engines if o != e and self.cnt[o] > 0])

    def wait_all(self, e, bufs):
        if self.dry:
            return
        deps = []
        for b in bufs:
            deps.append(b.writer)
            deps.extend(b.readers)
        self._need(e, deps)


class Ring:
    def __init__(self, items):
        self.items = items
        self.i = 0

    def next(self):
        r = self.items[self.i % len(self.items)]
        self.i += 1
        return r


def build_program(nlayers=NL, nseq=2, stage=99, debug=False):
    nc = bass.Bass("TRN2", target_bir_lowering=False)
    dt_in = lambda name, shape: nc.dram_tensor(name, shape, F32, kind="ExternalInput").ap()
    x_d = dt_in("x", [2, SEQ, D])
    wup_d = {1: dt_in("wup1", [NL, 11, 128, 4096]), 2: dt_in("wup2", [NL, 11, 128, 4096])}
    wdn_d = {1: dt_in("wdn1", [NL, 8, 128, 2816]), 2: dt_in("wdn2", [NL, 8, 128, 2816])}
    wlx_d = dt_in("wlx", [NL, 4, 128, 4096])
    wqkv_d = dt_in("wqkv", [NL, 8, 128, 3072])
    wf_d = dt_in("wf", [NL, 128, 128])
    wax_d = dt_in("wax", [NL, 8, 128, 256])
    wmg_d = dt_in("wmg", [NL, 8, 128, 4096])
    wo_d = dt_in("wo", [NL, 2, 128, 4096])
    pp_d = dt_in("pp", [128, NL * NPP])
    cst_d = dt_in("cst", [128, 5 * 128])
    y_d = nc.dram_tensor("y", [2, SEQ, D], F32, kind="ExternalOutput").ap()
    hr_d = nc.dram_tensor("hr_scratch", [KC, 128, SEQ], BF16, kind=("ExternalOutput" if debug else "Internal")).ap()

    with ExitStack() as st:
        S = Sync(nc, st)
        ARENA = 106400
        arena = st.enter_context(nc.sbuf_tensor("arena", [128, ARENA], BF16))
        psum = st.enter_context(nc.psum_tensor("psum", [128, 8, 512], F32))

        def vb(off, n):
            return arena[:, off:off + n]

        def vf(off, n):
            assert off % 2 == 0
            return arena[:, off:off + 2 * n].bitcast(F32)

        O_X, O_H, O_OA, O_HR, O_W, O_C, O_S = 0, 32768, 49152, 65536, 81920, 94208, 95808
        X = vf(O_X, 8 * SEQ).rearrange("p (c t) -> p c t", c=8)
        H = vb(O_H, 8 * SEQ).rearrange("p (c t) -> p c t", c=8)
        OA = vb(O_OA, 8 * SEQ).rearrange("p (c t) -> p c t", c=8)
        HR = vb(O_HR, 8 * SEQ).rearrange("p (c t) -> p c t", c=8)
        A = vb(O_OA, JC * 1024).rearrange("p (j t) -> p j t", j=JC)
        WS = [vb(O_W + i * 4096, 4096) for i in range(3)]
        o = O_C
        PP = vf(o, NL * NPP); o += 2 * NL * NPP
        DV = vf(o, NL * NDV); o += 2 * NL * NDV
        IDF = vf(o, 128); o += 256
        ONF = vf(o, 128); o += 256
        IDB = vb(o, 128); o += 128
        MKB = vb(o, 128); o += 128
        BDB = vb(o, 128); o += 128
        assert o <= O_S, o
        NSCR = ARENA - O_S

        XB = [[Buf("X%d_%d" % (c, t)) for t in range(NTB)] for c in range(KC)]
        HB = [[Buf("H%d_%d" % (c, t)) for t in range(NTB)] for c in range(KC)]
        OAB = [[Buf("OA%d_%d" % (c, t)) for t in range(NTB)] for c in range(KC)]
        HRB = [[Buf("HR%d_%d" % (c, t)) for t in range(NTB)] for c in range(KC)]
        AB = [[Buf("A%d_%d" % (j, t)) for t in range(2)] for j in range(JC)]
        WB = [Buf("W%d" % i) for i in range(3)]
        CB = Buf("const")
        CBP = Buf("constp")
        PB = [Buf("ps%d" % i, excl=True) for i in range(8)]
        PSA = Ring([(psum[:, i, :], PB[i]) for i in range(6)])
        PSB = Ring([(psum[:, i, :], PB[i]) for i in (6, 7)])
        YB = Buf("y")
        HRD = [Buf("hrd%d" % c) for c in range(KC)]
        SCRB = {}
        NSTAT = 4096
        STAT = {"sq": (vf(O_S, TB), Buf("st_sq")), "acc": (vf(O_S + 1024, TB), Buf("st_acc")),
                "rs": [(vf(O_S + 2048, TB), Buf("st_rs0")), (vf(O_S + 3072, TB), Buf("st_rs1"))]}
        STG = [Buf("stg0"), Buf("stg1")]

        def pp(l, col, n=1, p0=0, p1=128):
            return PP[p0:p1, l * NPP + col: l * NPP + col + n]

        def dv(l, col, n=1, p0=0, p1=128):
            return DV[p0:p1, l * NDV + col: l * NDV + col + n]

        class WStream:
            def __init__(self):
                self.sched = []
                self.pos = 0
                self.issued = 0

            def get(self, src, n, la=2):
                if S.dry:
                    self.sched.append((src, n))
                    return WS[0], WB[0]
                i = self.pos
                self.pos += 1
                while self.issued < min(i + 1 + la, len(self.sched)):
                    s_ap, s_n = self.sched[self.issued]
                    k = self.issued % 3
                    S.dma("pool", WS[k][:, 0:s_n], s_ap, writes=[WB[k]])
                    self.issued += 1
                return WS[i % 3], WB[i % 3]

        W = WStream()

        def mm(out, lhsT, rhs, start, stop, reads, writes, inc):
            S.op("pe", lambda: nc.tensor.matmul(out, lhsT=lhsT, rhs=rhs, start=start, stop=stop), reads, writes, inc)

        def tsl(tb):
            return slice(tb * TB, (tb + 1) * TB)

        class Scr:
            def __init__(self, regions):
                self.regions = list(regions)
                self.cur = 0
                self.off = self.regions[0][0]

            def _take(self, n):
                n = (n + 15) // 16 * 16
                while True:
                    base, size = self.regions[self.cur]
                    if self.off + n <= base + size:
                        r = self.off
                        self.off += n
                        return r
                    self.cur += 1
                    assert self.cur < len(self.regions), "scratch overflow"
                    self.off = self.regions[self.cur][0]

            def f32(self, name, n):
                off = self._take(2 * n)
                return vf(off, n), SCRB.setdefault(off, Buf(name))

            def b16(self, name, n):
                off = self._take(n)
                return vb(off, n), SCRB.setdefault(off, Buf(name))

        def load_consts():
            S.dma("sp", PP[:, :], pp_d[:, :], writes=[CB])
            S.dma("sp", IDF[:, :], cst_d[:, 0:128], writes=[CB])
            S.dma("sp", ONF[:, :], cst_d[:, 128:256], writes=[CB])
            S.dma("pool", BDB[:, :], cst_d[:, 256:384], writes=[CBP])
            S.dma("pool", IDB[:, :], cst_d[:, 384:512], writes=[CBP])
            S.dma("pool", MKB[:, :], cst_d[:, 512:640], writes=[CBP])
            for l in range(NL):
                S.op("act", lambda: nc.scalar.activation(out=dv(l, DV_COEF, 8), in_=pp(l, PP_LAM, 8), func=AF.Exp, scale=-1.0),
                     reads=[CB], writes=[CB])
                S.op("act", lambda: nc.scalar.activation(out=dv(l, DV_COEF, 8), in_=dv(l, DV_COEF, 8), func=AF.Ln, bias=1.0),
                     reads=[CB], writes=[CB])
                S.op("dve", lambda: nc.vector.tensor_scalar(out=dv(l, DV_HCOEF, 8), in0=dv(l, DV_COEF, 8), scalar1=-4.0, scalar2=None, op0=ALU.mult),
                     reads=[CB], writes=[CB])
                S.op("dve", lambda: nc.vector.tensor_scalar(out=dv(l, DV_COEF, 8), in0=dv(l, DV_COEF, 8), scalar1=-8.0, scalar2=None, op0=ALU.mult),
                     reads=[CB], writes=[CB])
                S.op("dve", lambda: nc.vector.tensor_scalar(out=dv(l, DV_HBA, 8), in0=pp(l, PP_BA, 8), scalar1=0.5, scalar2=None, op0=ALU.mult),
                     reads=[CB], writes=[CB])
                S.op("dve", lambda: nc.vector.tensor_scalar(out=dv(l, DV_HBX, 8), in0=pp(l, PP_BX, 8), scalar1=0.5, scalar2=None, op0=ALU.mult),
                     reads=[CB], writes=[CB])
                S.op("dve", lambda: nc.vector.tensor_scalar(out=dv(l, DV_GQS, 1), in0=pp(l, PP_GQ, 1), scalar1=0.125, scalar2=None, op0=ALU.mult),
                     reads=[CB], writes=[CB])
                S.op("dve", lambda: nc.vector.tensor_scalar(out=dv(l, DV_NBF, 1), in0=pp(l, PP_BF, 1), scalar1=-1.0, scalar2=None, op0=ALU.mult),
                     reads=[CB], writes=[CB])

        def load_x(s):
            stg = [(vf(O_OA + i * 8192, 4096).rearrange("p (b f) -> p b f", b=4), STG[i]) for i in range(2)]
            for tb in range(NTB):
                sv, sbuf = stg[tb % 2]
                src = x_d[s, tb * TB:(tb + 1) * TB, :].rearrange("(b p) f -> p b f", p=128)
                S.dma("sp", sv, src, writes=[sbuf])
                for c in range(KC):
                    bank, bb = PSA.next()
                    for b in range(4):
                        S.op("pe", lambda: nc.tensor.transpose(bank[:, b * 128:(b + 1) * 128], sv[:, b, c * 128:(c + 1) * 128], IDF[:, :]),
                             reads=[sbuf, CB], writes=[bb], inc=(b == 3))
                    if c % 2 == 0:
                        S.op("act", lambda: nc.scalar.activation(out=X[:, c, tsl(tb)], in_=bank, func=AF.Copy), reads=[bb], writes=[XB[c][tb]])
                    else:
                        S.op("dve", lambda: nc.vector.tensor_copy(out=X[:, c, tsl(tb)], in_=bank), reads=[bb], writes=[XB[c][tb]])

        def store_x(s):
            stg = [(vf(O_OA + i * 8192, 4096).rearrange("p (b f) -> p b f", b=4), STG[i]) for i in range(2)]
            for tb in range(NTB):
                sv, sbuf = stg[tb % 2]
                for b in range(4):
                    for cg in range(2):
                        bank, bb = PSA.next()
                        for c4 in range(4):
                            c = cg * 4 + c4
                            S.op("pe", lambda: nc.tensor.transpose(bank[:, c4 * 128:(c4 + 1) * 128],
                                                                   X[:, c, tb * TB + b * 128: tb * TB + (b + 1) * 128], IDF[:, :]),
                                 reads=[XB[c][tb], CB], writes=[bb], inc=(c4 == 3))
                        if (b + cg) % 2 == 0:
                            S.op("act", lambda: nc.scalar.activation(out=sv[:, b, cg * 512:(cg + 1) * 512], in_=bank, func=AF.Copy), reads=[bb], writes=[sbuf])
                        else:
                            S.op("dve", lambda: nc.vector.tensor_copy(out=sv[:, b, cg * 512:(cg + 1) * 512], in_=bank), reads=[bb], writes=[sbuf])
                dst = y_d[s, tb * TB:(tb + 1) * TB, :].rearrange("(b p) f -> p b f", p=128)
                S.dma("sp", dst, sv, reads=[sbuf], writes=[], dbuf=YB)

        def rms_p1(tb):
            accv, accb = STAT["acc"]
            sqv, sqb = STAT["sq"]
            for c in range(KC):
                if c == 0:
                    S.op("act", lambda: nc.scalar.activation(out=accv, in_=X[:, 0, tsl(tb)], func=AF.Square),
                         reads=[XB[0][tb]], writes=[accb])
                else:
                    S.op("act", lambda: nc.scalar.activation(out=sqv, in_=X[:, c, tsl(tb)], func=AF.Square),
                         reads=[XB[c][tb]], writes=[sqb])
                    S.op("dve", lambda: nc.vector.tensor_tensor(out=accv, in0=accv, in1=sqv, op=ALU.add),
                         reads=[sqb, accb], writes=[accb])

        def rms_p2(tb):
            accv, accb = STAT["acc"]
            rsv, rsb = STAT["rs"][tb % 2]
            bank, bb = PSA.next()
            mm(bank, ONF[:, :], accv, True, True, [accb, CB], [bb], True)
            S.op("act", lambda: nc.scalar.activation(out=rsv, in_=bank, func=AF.Ln, scale=1.0 / D, bias=EPS), reads=[bb], writes=[rsb])
            S.op("act", lambda: nc.scalar.activation(out=rsv, in_=rsv, func=AF.Exp, scale=-0.5), reads=[rsb], writes=[rsb])

        def rms_p3(l, gcol, tb):
            rsv, rsb = STAT["rs"][tb % 2]
            for c in range(KC):
                S.op("dve", lambda: nc.vector.scalar_tensor_tensor(out=H[:, c, tsl(tb)], in0=X[:, c, tsl(tb)], scalar=pp(l, gcol + c),
                                                                   in1=rsv, op0=ALU.mult, op1=ALU.mult),
                     reads=[XB[c][tb], rsb, CB], writes=[HB[c][tb]])

        def rmsnorm_tb(l, gcol, tb):
            rms_p1(tb)
            rms_p2(tb)
            rms_p3(l, gcol, tb)

        class Hooks:
            def __init__(self, items):
                self.steps = []
                for (tb, l, gcol) in items:
                    self.steps += [lambda tb=tb: rms_p1(tb), lambda tb=tb: rms_p2(tb), lambda tb=tb, l=l, gcol=gcol: rms_p3(l, gcol, tb)]

            def step(self):
                if self.steps:
                    self.steps.pop(0)()

            def flush(self):
                while self.steps:
                    self.steps.pop(0)()

        def ffn(l, which, tt, have_h, hooks):
            gcol = PP_GF1 if which == 1 else PP_GF2
            sc = Scr([(O_S + NSTAT, NSCR - NSTAT)])
            sg = Ring([sc.f32("sg%d" % i, TB) for i in range(3)])
            tbs = [2 * tt, 2 * tt + 1]
            if not have_h:
                for tb in tbs:
                    rmsnorm_tb(l, gcol, tb)
            hk = Hooks(hooks)
            for jj in range(11):
                wt, wb = W.get(wup_d[which][l, jj], 4096)
                wv = wt.rearrange("p (a k n) -> p a k n", a=2, k=8)
                for j2 in range(2):
                    j = 2 * jj + j2
                    for tl, tb in enumerate(tbs):
                        bg, bgb = PSA.next()
                        for k in range(KC):
                            mm(bg, wv[:, j2, k, 0:128], H[:, k, tsl(tb)], k == 0, k == KC - 1, [wb, HB[k][tb]], [bgb], k == KC - 1)
                        bu, bub = PSA.next()
                        for k in range(KC):
                            mm(bu, wv[:, j2, k, 128:256], H[:, k, tsl(tb)], k == 0, k == KC - 1, [wb, HB[k][tb]], [bub], k == KC - 1)
                        sgv, sgb = sg.next()
                        S.op("act", lambda: nc.scalar.activation(out=sgv, in_=bg, func=AF.Silu), reads=[bgb], writes=[sgb])
                        S.op("dve", lambda: nc.vector.tensor_tensor(out=A[:, j, tl * TB:(tl + 1) * TB], in0=sgv, in1=bu, op=ALU.mult),
                             reads=[sgb, bub], writes=[AB[j][tl]])
                if jj >= 1:
                    hk.step()
            for oc in range(KC):
                wt, wb = W.get(wdn_d[which][l, oc], 2816)
                wv = wt[:, 0:2816].rearrange("p (j n) -> p j n", j=JC)
                for tl, tb in enumerate(tbs):
                    bd, bdb = PSA.next()
                    for j in range(JC):
                        mm(bd, wv[:, j, :], A[:, j, tl * TB:(tl + 1) * TB], j == 0, j == JC - 1, [wb, AB[j][tl]], [bdb], j == JC - 1)
                    S.op("dve", lambda: nc.vector.scalar_tensor_tensor(out=X[:, oc, tsl(tb)], in0=bd, scalar=0.5, in1=X[:, oc, tsl(tb)],
                                                                       op0=ALU.mult, op1=ALU.add),
                         reads=[bdb, XB[oc][tb]], writes=[XB[oc][tb]])
            hk.flush()

        def branch_a(l):
            scA = Scr([(O_S, NSCR)])
            scB = Scr([(O_OA, 16 * SEQ)])
            NP = NTB
            pcs = range(NP)
            ZX = [scA.f32("zx%d" % i, SEQ + 4)[0] for i in range(2)]
            ZXb = [[Buf("zx%d_%d" % (i, p)) for p in pcs] for i in range(2)]
            WAX = [scA.b16("wax%d" % i, 256) for i in range(2)]
            XR = [scB.f32("xr%d" % i, SEQ)[0] for i in range(2)]
            XRb = [[Buf("xr%d_%d" % (i, p)) for p in pcs] for i in range(2)]
            TRv = scB.f32("tr", SEQ)[0]; TRb = [Buf("tr%d" % p) for p in pcs]
            TIv = scB.f32("ti", SEQ)[0]; TIb = [Buf("ti%d" % p) for p in pcs]
            Qv = scB.f32("q", SEQ)[0]; Qb = [Buf("q%d" % p) for p in pcs]
            HSv = scB.f32("hs", SEQ)[0]; HSb = [Buf("hs%d" % p) for p in pcs]
            XBv = scB.b16("xrb", SEQ)[0]; XBb = [Buf("xb%d" % p) for p in pcs]
            HRO = [scB.b16("hro%d" % i, SEQ) for i in range(2)]
            Q0 = psum[:, 0:4, :]
            Q1 = psum[:, 4:8, :]
            QB0 = PB[0:4]
            QB1 = PB[4:8]
            for i in range(2):
                S.op("dve", lambda: nc.vector.memset(ZX[i][:, 0:3], 0.0), writes=[ZXb[i][0]])
            wts = {}

            def stage_a(c):
                cc, c2 = c // 2, c % 2
                if c2 == 0:
                    wt, wb = W.get(wlx_d[l, cc], 4096, la=1)
                    wts[cc] = (wt.rearrange("p (a k n) -> p a k n", a=2, k=8), wb)
                wv, wb = wts[cc]
                zxv, zxb = ZX[c % 2], ZXb[c % 2]
                xrv, xrb = XR[c % 2], XRb[c % 2]
                for p in pcs:
                    for k in range(KC):
                        mm(Q0[:, p, :], wv[:, c2, k, 0:128], H[:, k, tsl(p)], k == 0, k == KC - 1, [wb, HB[k][p]], [QB0[p]], k == KC - 1)
                for p in pcs:
                    S.op("act", lambda: nc.scalar.activation(out=zxv[:, 3 + p * TB:3 + (p + 1) * TB], in_=Q0[:, p, :], func=AF.Copy),
                         reads=[QB0[p]], writes=[zxb[p]])
                for p in pcs:
                    zr = [zxb[p]] + ([zxb[p - 1]] if p else [])
                    S.op("dve", lambda: nc.vector.tensor_scalar(out=xrv[:, tsl(p)], in0=zxv[:, p * TB:(p + 1) * TB], scalar1=pp(l, PP_CW + 4 * c + 0),
                                                                scalar2=pp(l, PP_CB + c), op0=ALU.mult, op1=ALU.add), reads=zr + [CB], writes=[xrb[p]])
                    for j in range(1, 4):
                        S.op("dve", lambda: nc.vector.scalar_tensor_tensor(out=xrv[:, tsl(p)], in0=zxv[:, p * TB + j:(p + 1) * TB + j],
                                                                           scalar=pp(l, PP_CW + 4 * c + j), in1=xrv[:, tsl(p)], op0=ALU.mult, op1=ALU.add),
                             reads=zr + [CB, xrb[p]], writes=[xrb[p]])

            def stage_b(c):
                cc, c2 = c // 2, c % 2
                wv, wb = wts[cc]
                xrv, xrb = XR[c % 2], XRb[c % 2]
                waxv, waxb = WAX[c % 2]
                S.dma("pool", waxv, wax_d[l, c], writes=[waxb])
                for p in pcs:
                    S.op("act", lambda: nc.scalar.activation(out=XBv[:, tsl(p)], in_=xrv[:, tsl(p)], func=AF.Copy), reads=[xrb[p]], writes=[XBb[p]])
                for p in pcs:
                    mm(Q1[:, p, :], waxv[:, 0:128], XBv[:, tsl(p)], True, True, [waxb, XBb[p]], [QB1[p]], True)
                for p in pcs:
                    S.op("act", lambda: nc.scalar.activation(out=TRv[:, tsl(p)], in_=Q1[:, p, :], func=AF.Tanh, scale=0.5, bias=dv(l, DV_HBA + c)),
                         reads=[QB1[p], CB], writes=[TRb[p]])
                for p in pcs:
                    mm(Q0[:, p, :], waxv[:, 128:256], XBv[:, tsl(p)], True, True, [waxb, XBb[p]], [QB0[p]], True)
                for p in pcs:
                    S.op("act", lambda: nc.scalar.activation(out=TIv[:, tsl(p)], in_=Q0[:, p, :], func=AF.Tanh, scale=0.5, bias=dv(l, DV_HBX + c)),
                         reads=[QB0[p], CB], writes=[TIb[p]])
                for p in pcs:
                    S.op("dve", lambda: nc.vector.scalar_tensor_tensor(out=TIv[:, tsl(p)], in0=TIv[:, tsl(p)], scalar=1.0, in1=xrv[:, tsl(p)],
                                                                       op0=ALU.add, op1=ALU.mult), reads=[TIb[p], xrb[p]], writes=[TIb[p]])
                for p in pcs:
                    S.op("act", lambda: nc.scalar.activation(out=Qv[:, tsl(p)], in_=TRv[:, tsl(p)], func=AF.Exp, scale=dv(l, DV_COEF + c),
                                                             bias=dv(l, DV_COEF + c)), reads=[TRb[p], CB], writes=[Qb[p]])
                for p in pcs:
                    S.op("act", lambda: nc.scalar.activation(out=TRv[:, tsl(p)], in_=TRv[:, tsl(p)], func=AF.Exp, scale=dv(l, DV_HCOEF + c),
                                                             bias=dv(l, DV_HCOEF + c)), reads=[TRb[p], CB], writes=[TRb[p]])
                for p in pcs:
                    for k in range(KC):
                        mm(Q1[:, p, :], wv[:, c2, k, 128:256], H[:, k, tsl(p)], k == 0, k == KC - 1, [wb, HB[k][p]], [QB1[p]], k == KC - 1)
                for p in pcs:
                    S.op("act", lambda: nc.scalar.activation(out=Qv[:, tsl(p)], in_=Qv[:, tsl(p)], func=AF.Sqrt, scale=-0.25, bias=0.25),
                         reads=[Qb[p]], writes=[Qb[p]])
                for p in pcs:
                    S.op("dve", lambda: nc.vector.tensor_tensor(out=TIv[:, tsl(p)], in0=TIv[:, tsl(p)], in1=Qv[:, tsl(p)], op=ALU.mult),
                         reads=[TIb[p], Qb[p]], writes=[TIb[p]])
                    init = 0.0 if p == 0 else HSv[:, p * TB - 1:p * TB]
                    S.op("dve", lambda: nc.vector.tensor_tensor_scan(out=HSv[:, tsl(p)], data0=TRv[:, tsl(p)], data1=TIv[:, tsl(p)], initial=init,
                                                                     op0=ALU.mult, op1=ALU.add),
                         reads=[TRb[p], TIb[p]] + ([HSb[p - 1]] if p else []), writes=[HSb[p]])
                hov, hob = HRO[c % 2]
                for p in pcs:
                    S.op("act", lambda: nc.scalar.activation(out=Qv[:, tsl(p)], in_=Q1[:, p, :], func=AF.Gelu_apprx_tanh), reads=[QB1[p]], writes=[Qb[p]])
                for p in pcs:
                    S.op("dve", lambda: nc.vector.tensor_tensor(out=hov[:, tsl(p)], in0=Qv[:, tsl(p)], in1=HSv[:, tsl(p)], op=ALU.mult),
                         reads=[Qb[p], HSb[p]], writes=[hob])
                S.dma("sp", hr_d[c], hov, reads=[hob], writes=[HRD[c]])

            stage_a(0)
            for c in range(KC):
                if c + 1 < KC:
                    stage_a(c + 1)
                stage_b(c)
            S.barrier()

        def mixer(l, m1_tbs, merge_hooks):
            for tb in m1_tbs:
                rmsnorm_tb(l, PP_GMIX, tb)
            S.barrier()
            branch_a(l)
            sc = Scr([(O_S + NSTAT, NSCR - NSTAT)])
            sq = [STAT["sq"], STAT["acc"]]
            rs = STAT["rs"]
            PT = Ring([sc.b16("pt%d" % i, TB) for i in range(5)])
            RC = Ring([sc.f32("rc%d" % i, TB) for i in range(2)])
            WFv, WFb = sc.b16("wf", 128)
            S.dma("pool", WFv, wf_d[l], writes=[WFb])
            WFk = WFv.rearrange("p (k n) -> p k n", k=8)
            sch = Scr([(O_HR, 8 * SEQ)])
            qa = [sch.b16("qaA", SEQ), sch.b16("qaB", SEQ)]
            ka = [sch.b16("kaA", SEQ), sch.b16("kaB", SEQ)]
            qa2 = [sch.b16("qaA2", SEQ), sch.b16("qaB2", SEQ)]
            QA = [qa, qa2]
            Vpv, Vpb = sch.b16("vp", 16 * 192)
            Vp = Vpv.rearrange("p (b n) -> p b n", b=16)
            CSP = [qa[0][0], qa[1][0], ka[0][0]]
            CSb = Buf("cs")
            QAUG = [[Buf("qaug0"), Buf("qaug1")], [Buf("qaug2"), Buf("qaug3")]]
            KAUG = [Buf("kaug0"), Buf("kaug1")]
            P0, P1 = 96, 112
            sc2 = Scr([(O_OA, 8 * SEQ)])
            Ev, Eb = sc2.f32("E", SEQ)
            Lv, Lb = sc2.f32("Lg", SEQ)
            Cv, Cb_ = sc2.f32("cum", SEQ)
            ON1v, ON1b = sc2.f32("ones", TB)
            S.op("pool", lambda: nc.gpsimd.memset(ON1v, 1.0), writes=[ON1b])
            for tb in range(NTB):
                bank, bb = PSA.next()
                for k in range(KC):
                    mm(bank[0:16, :], WFk[:, k, :], H[:, k, tsl(tb)], k == 0, k == KC - 1, [WFb, HB[k][tb]], [bb], k == KC - 1)
                S.op("act", lambda: nc.scalar.activation(out=Ev[0:16, tsl(tb)], in_=bank[0:16, :], func=AF.Exp, scale=-1.0,
                                                         bias=dv(l, DV_NBF, 1, 0, 16)), reads=[bb, CB], writes=[Eb])
            S.op("act", lambda: nc.scalar.activation(out=Lv[P0:P1, :], in_=Ev[0:16, :], func=AF.Ln, bias=1.0), reads=[Eb], writes=[Lb])
            for tb in range(NTB):
                init = 0.0 if tb == 0 else Cv[P0:P1, tb * TB - 1: tb * TB]
                S.op("dve", lambda: nc.vector.tensor_tensor_scan(out=Cv[P0:P1, tsl(tb)], data0=ON1v[P0:P1, :], data1=Lv[P0:P1, tsl(tb)],
                                                                 initial=init, op0=ALU.mult, op1=ALU.subtract),
                     reads=[ON1b, Lb, Cb_], writes=[Cb_])
            S.op("dve", lambda: nc.vector.tensor_copy(out=CSP[0][P0:P1, :], in_=Cv[P0:P1, :]), reads=[Cb_], writes=[CSb])
            S.op("dve", lambda: nc.vector.tensor_tensor(out=Ev[P0:P1, :], in0=Cv[P0:P1, :], in1=CSP[0][P0:P1, :], op=ALU.subtract),
                 reads=[Cb_, CSb, Eb], writes=[Eb])
            S.op("dve", lambda: nc.vector.tensor_copy(out=CSP[1][P0:P1, :], in_=Ev[P0:P1, :]), reads=[Eb], writes=[CSb])
            S.op("dve", lambda: nc.vector.tensor_tensor(out=Lv[P0:P1, :], in0=Ev[P0:P1, :], in1=CSP[1][P0:P1, :], op=ALU.subtract),
                 reads=[Eb, CSb, Lb], writes=[Lb])
            S.op("dve", lambda: nc.vector.tensor_copy(out=CSP[2][P0:P1, :], in_=Lv[P0:P1, :]), reads=[Lb], writes=[CSb])
            for hh in range(2):
                S.op("pool", lambda: nc.gpsimd.memset(qa[hh][0][64:70, :], -1.0), writes=[QAUG[0][hh]])
                S.op("pool", lambda: nc.gpsimd.memset(qa2[hh][0][64:70, :], -1.0), writes=[QAUG[1][hh]])
                S.op("pool", lambda: nc.gpsimd.memset(ka[hh][0][64:70, :], 1.0), writes=[KAUG[hh]])
            S.op("pool", lambda: nc.gpsimd.memset(Vp[:, :, 64:128], 1.0), writes=[Vpb])

            SQB = [sc.b16("sqb%d" % i, TB) for i in range(2)]
            wq = {}

            def get_w(c):
                if c not in wq:
                    wt, wb = W.get(wqkv_d[l, c], 3072, la=1)
                    wq[c] = (wt[:, 0:3072].rearrange("p (k n) -> p k n", k=8), wb)
                return wq[c]

            def proj_norm(c, which, dst, gcol_ap):
                wv, wb = get_w(c)
                for tb in range(NTB):
                    bq, bqb = PSA.next()
                    for k in range(KC):
                        mm(bq, wv[:, k, which * 128:(which + 1) * 128], H[:, k, tsl(tb)], k == 0, k == KC - 1, [wb, HB[k][tb]], [bqb], k == KC - 1)
                    sqv, sqb = SQB[tb % 2]
                    S.op("act", lambda: nc.scalar.activation(out=sqv, in_=bq, func=AF.Square), reads=[bqb], writes=[sqb])
                    bs, bsb = PSA.next()
                    mm(bs, BDB[:, :], sqv, True, True, [sqb, CBP], [bsb], True)
                    rsv, rsb = rs[tb % 2]
                    S.op("act", lambda: nc.scalar.activation(out=rsv, in_=bs, func=AF.Ln, scale=1.0 / HD, bias=EPS), reads=[bsb], writes=[rsb])
                    S.op("act", lambda: nc.scalar.activation(out=rsv, in_=rsv, func=AF.Exp, scale=-0.5), reads=[rsb], writes=[rsb])
                    for hh in range(2):
                        p0, p1 = hh * 64, hh * 64 + 64
                        S.op("dve", lambda: nc.vector.scalar_tensor_tensor(out=dst[hh][0][0:64, tsl(tb)], in0=bq[p0:p1, :], scalar=gcol_ap(p0, p1),
                                                                           in1=rsv[p0:p1, :], op0=ALU.mult, op1=ALU.mult),
                             reads=[bqb, rsb, CB], writes=[dst[hh][1]])

            gq_ap = lambda p0, p1: dv(l, DV_GQS, 1, p0, p1)
            gk_ap = lambda p0, p1: pp(l, PP_GK, 1, p0, p1)

            def proj_v(c):
                wv, wb = get_w(c)
                for kg in range(4):
                    bv, bvb = PSA.next()
                    for kbl in range(4):
                        kb = 4 * kg + kbl
                        for k in range(KC):
                            mm(bv[:, kbl * 128:(kbl + 1) * 128], H[:, k, kb * 128:(kb + 1) * 128], wv[:, k, 256:384], k == 0, k == KC - 1,
                               [wb, HB[k][kg]], [bvb], (k == KC - 1 and kbl == 3))
                    src = bv.rearrange("p (b h d) -> p b h d", b=4, h=2)
                    dstv = Vp[:, 4 * kg:4 * kg + 4, :].rearrange("p b (h d) -> p b h d", h=3)[:, :, 0:3:2, :]
                    S.op("act", lambda: nc.scalar.activation(out=dstv, in_=src, func=AF.Copy), reads=[bvb], writes=[Vpb])

            def aug_q(c, qb_):
                for hh in range(2):
                    h = 2 * c + hh
                    for i in range(3):
                        S.dma("sp", QA[qb_][hh][0][64 + i:65 + i, :], CSP[i][P0 + h:P0 + h + 1, :], reads=[CSb], writes=[QAUG[qb_][hh]])

            def aug_k(c):
                for hh in range(2):
                    h = 2 * c + hh
                    for i in range(3):
                        S.dma("sp", ka[hh][0][67 + i:68 + i, :], CSP[i][P0 + h:P0 + h + 1, :], reads=[CSb], writes=[KAUG[hh]])

            def head_tiles(c, hh, qb_):
                qav, qab = QA[qb_][hh]
                kav, kab = ka[hh]
                tiles = [(qb, kb) for qb in range(NTB) for kb in range(4 * qb + 4)]
                LA = 4
                pend = []
                bo = {}
                for idx in range(len(tiles) + LA):
                    if idx < len(tiles):
                        qb, kb = tiles[idx]
                        i = kb - 4 * qb
                        n0 = 128 * max(i, 0)
                        N = TB - n0
                        bs, bsb = PSA.next()
                        mm(bs[:, 0:N], kav[0:70, kb * 128:(kb + 1) * 128], qav[0:70, qb * TB + n0:(qb + 1) * TB], True, i < 0,
                           [kab, qab, QAUG[qb_][hh], KAUG[hh]], [bsb], i < 0)
                        if i >= 0:
                            mm(bs[:, 0:128], IDB[:, :], MKB[:, :], False, True, [CBP], [bsb], True)
                        ptv, ptb = PT.next()
                        S.op("act", lambda: nc.scalar.activation(out=ptv[:, 0:N], in_=bs[:, 0:N], func=AF.Exp), reads=[bsb], writes=[ptb])
                        pend.append((qb, kb, n0, N, ptv, ptb))
                    if idx >= LA:
                        qb, kb, n0, N, ptv, ptb = pend.pop(0)
                        if kb == 0:
                            bo[qb] = PSB.next()
                        bov, bob = bo[qb]
                        last = (kb == 4 * qb + 3)
                        mm(bov[:, n0:TB], Vp[:, kb, hh * 64: hh * 64 + 128], ptv[:, 0:N], kb == 0, last, [Vpb, ptb], [bob], last)
                        if last:
                            rcv, rcb = RC.next()
                            po, pd = (0, 64) if hh == 0 else (64, 0)
                            S.op("dve", lambda: nc.vector.reciprocal(out=rcv[po:po + 64, :], in_=bov[pd:pd + 64, :]), reads=[bob], writes=[rcb])
                            S.op("dve", lambda: nc.vector.tensor_tensor(out=OA[po:po + 64, c, tsl(qb)], in0=bov[po:po + 64, :], in1=rcv[po:po + 64, :],
                                                                        op=ALU.mult), reads=[bob, rcb], writes=[OAB[c][qb]])

            proj_norm(0, 0, QA[0], gq_ap)
            for c in range(KC):
                qb_ = c % 2
                proj_norm(c, 1, ka, gk_ap)
                proj_v(c)
                if c == 0:
                    S.barrier()
                    aug_q(0, 0)
                aug_k(c)
                head_tiles(c, 0, qb_)
                if c + 1 < KC:
                    proj_norm(c + 1, 0, QA[1 - qb_], gq_ap)
                    aug_q(c + 1, 1 - qb_)
                head_tiles(c, 1, qb_)
            S.barrier()
            alias = [qa[0][1], qa[1][1], qa2[0][1], qa2[1][1], ka[0][1], ka[1][1], Vpb, QAUG[0][0], QAUG[0][1], QAUG[1][0], QAUG[1][1], KAUG[0], KAUG[1], CSb]
            for tb in range(NTB):
                S.dma("sp", HR[:, :, tsl(tb)], hr_d[:, :, tsl(tb)].rearrange("c p t -> p c t"), reads=HRD,
                      writes=[HRB[c][tb] for c in range(KC)] + (alias if tb == 0 else []))

            sc = Scr([(O_S + NSTAT, NSCR - NSTAT)])
            Mv, _ = sc.b16("m", 8 * TB)
            M = Mv.rearrange("p (c t) -> p c t", c=8)
            MB = [Buf("m%d" % c) for c in range(KC)]
            GL = Ring([sc.f32("gl0", TB)])
            GA = Ring([sc.f32("ga0", TB)])
            hk = Hooks(merge_hooks)
            for tb in range(NTB):
                for oc in range(KC):
                    wt, wb = W.get(wmg_d[l, oc], 4096)
                    wv = wt.rearrange("p (k n) -> p k n", k=8)
                    bgl, bglb = PSA.next()
                    for k in range(KC):
                        mm(bgl, wv[:, k, 256:384], H[:, k, tsl(tb)], k == 0, k == KC - 1, [wb, HB[k][tb]], [bglb], k == KC - 1)
                    bya, byab = PSA.next()
                    for k in range(KC):
                        mm(bya, wv[:, k, 128:256], OA[:, k, tsl(tb)], k == 0, k == KC - 1, [wb, OAB[k][tb]], [byab], k == KC - 1)
                    bga, bgab = PSA.next()
                    for k in range(KC):
                        mm(bga, wv[:, k, 384:512], H[:, k, tsl(tb)], k == 0, k == KC - 1, [wb, HB[k][tb]], [bgab], k == KC - 1)
                    byl, bylb = PSA.next()
                    for k in range(KC):
                        mm(byl, wv[:, k, 0:128], HR[:, k, tsl(tb)], k == 0, k == KC - 1, [wb, HRB[k][tb]], [bylb], k == KC - 1)
                    glv, glb = GL.next()
                    gav, gab = GA.next()
                    S.op("act", lambda: nc.scalar.activation(out=glv, in_=bgl, func=AF.Sigmoid, bias=pp(l, PP_BG + oc)), reads=[bglb, CB], writes=[glb])
                    S.op("act", lambda: nc.scalar.activation(out=gav, in_=bga, func=AF.Sigmoid, bias=pp(l, PP_BG + 8 + oc)), reads=[bgab, CB], writes=[gab])
                    S.op("dve", lambda: nc.vector.tensor_tensor(out=glv, in0=glv, in1=byl, op=ALU.mult), reads=[glb, bylb], writes=[glb])
                    S.op("dve", lambda: nc.vector.tensor_tensor(out=gav, in0=gav, in1=bya, op=ALU.mult), reads=[gab, byab], writes=[gab])
                    S.op("dve", lambda: nc.vector.tensor_tensor(out=M[:, oc, :], in0=glv, in1=gav, op=ALU.add), reads=[glb, gab], writes=[MB[oc]])
                    if tb >= 1 and oc >= 1 and oc % 2 == 1:
                        if (tb == 1 and len(hk.steps) > 3) or (tb >= 2):
                            hk.step()
                for og in range(2):
                    wt, wb = W.get(wo_d[l, og], 4096)
                    wv = wt.rearrange("p (a k n) -> p a k n", a=4, k=8)
                    for o4 in range(4):
                        oc = 4 * og + o4
                        bx, bxb = PSA.next()
                        for k in range(KC):
                            mm(bx, wv[:, o4, k, :], M[:, k, :], k == 0, k == KC - 1, [wb, MB[k]], [bxb], k == KC - 1)
                        S.op("dve", lambda: nc.vector.tensor_tensor(out=X[:, oc, tsl(tb)], in0=bx, in1=X[:, oc, tsl(tb)], op=ALU.add),
                             reads=[bxb, XB[oc][tb]], writes=[XB[oc][tb]])
            hk.flush()
            S.barrier()

        def emit_all():
            load_consts()
            for s_ in range(nseq):
                S.barrier()
                load_x(s_)
                S.barrier()
                for tb in (0, 1):
                    rmsnorm_tb(0, PP_GF1, tb)
                for l in range(nlayers):
                    ffn(l, 1, 0, True, [(2, l, PP_GF1), (3, l, PP_GF1)])
                    ffn(l, 1, 1, True, [(0, l, PP_GMIX), (1, l, PP_GMIX)] if stage > 1 else [])
                    if stage <= 1:
                        S.barrier()
                        break
                    mixer(l, (2, 3), [(0, l, PP_GF2), (1, l, PP_GF2)] if stage > 2 else [])
                    if stage <= 2:
                        break
                    last = (l + 1 == nlayers) or stage <= 3
                    ffn(l, 2, 0, True, [(2, l, PP_GF2), (3, l, PP_GF2)])
                    ffn(l, 2, 1, True, [] if last else [(0, l + 1, PP_GF1), (1, l + 1, PP_GF1)])
                    if last:
                        S.barrier()
                    if stage <= 3:
                        break
                store_x(s_)
            if not S.dry:
                S._need("sp", [(YB, YB.dcount)])

        block = st.enter_context(nc.Block())

        @block.sync
        def _(sync):
            S.dry = True
            emit_all()
            S.dry = False
            PSA.i = 0
            PSB.i = 0
            emit_all()
        build_program.last_ninst = S.ninst
        build_program.last_cnt = dict(S.cnt)
    return nc


def _prep_weights(inp):
    f = lambda a: np.ascontiguousarray(a, dtype=np.float32)

    def up_layout(w):
        g = w[:, :, :FF].reshape(NL, 8, 128, 11, 2, 128)
        u = w[:, :, FF:].reshape(NL, 8, 128, 11, 2, 128)
        t = np.stack([g, u], axis=-2)
        t = t.transpose(0, 3, 2, 4, 1, 5, 6)
        return f(t.reshape(NL, 11, 128, 4096))

    def dn_layout(w):
        t = w.reshape(NL, JC, 128, 8, 128).transpose(0, 3, 2, 1, 4)
        return f(t.reshape(NL, 8, 128, 2816))

    def cols(w, off, n):
        return w[:, :, off:off + n].reshape(NL, 8, 128, n)

    w_in = inp["w_in"]
    lx = cols(w_in, OFF_LX, 1024).reshape(NL, 8, 128, 4, 2, 128)
    lg = cols(w_in, OFF_LG, 1024).reshape(NL, 8, 128, 4, 2, 128)
    t = np.stack([lx, lg], axis=-2).transpose(0, 3, 2, 4, 1, 5, 6)
    wlx = f(t.reshape(NL, 4, 128, 4096))
    q = cols(w_in, OFF_Q, 1024).reshape(NL, 8, 128, 8, 128)
    k_ = cols(w_in, OFF_K, 1024).reshape(NL, 8, 128, 8, 128)
    v = cols(w_in, OFF_V, 1024).reshape(NL, 8, 128, 8, 128)
    t = np.stack([q, k_, v], axis=-2).transpose(0, 3, 2, 1, 4, 5)
    wqkv = f(t.reshape(NL, 8, 128, 3072))
    t = cols(w_in, OFF_F, 16).transpose(0, 2, 1, 3)
    wf = f(t.reshape(NL, 128, 128))
    wax = f(np.concatenate([inp["w_a"], inp["w_x"]], axis=-1))
    lo = inp["w_lru_out"].reshape(NL, 8, 128, 8, 128)
    ao = inp["w_attn_out"].reshape(NL, 8, 128, 8, 128)
    gl = cols(w_in, OFF_GATE, 1024).reshape(NL, 8, 128, 8, 128)
    ga = cols(w_in, OFF_GATE + 1024, 1024).reshape(NL, 8, 128, 8, 128)
    t = np.stack([lo, ao, gl, ga], axis=-2).transpose(0, 3, 2, 1, 4, 5)
    wmg = f(t.reshape(NL, 8, 128, 4096))
    t = inp["w_o"].reshape(NL, 8, 128, 2, 4, 128).transpose(0, 3, 2, 4, 1, 5)
    wo = f(t.reshape(NL, 2, 128, 4096))
    ppm = np.zeros((128, NL, NPP), np.float32)
    vec = lambda a: a.reshape(8, 128).T
    for l in range(NL):
        ppm[:, l, PP_GF1:PP_GF1 + 8] = vec(inp["g_ffn1"][l])
        ppm[:, l, PP_GMIX:PP_GMIX + 8] = vec(inp["g_mix"][l])
        ppm[:, l, PP_GF2:PP_GF2 + 8] = vec(inp["g_ffn2"][l])
        cw = inp["conv_w"][l].reshape(4, 8, 128).transpose(2, 1, 0)
        ppm[:, l, PP_CW:PP_CW + 32] = cw.reshape(128, 32)
        ppm[:, l, PP_CB:PP_CB + 8] = vec(inp["conv_b"][l])
        ppm[:, l, PP_BA:PP_BA + 8] = vec(inp["b_a"][l])
        ppm[:, l, PP_BX:PP_BX + 8] = vec(inp["b_x"][l])
        ppm[:, l, PP_LAM:PP_LAM + 8] = vec(inp["lam"][l])
        ppm[:, l, PP_BG:PP_BG + 16] = inp["b_gate"][l].reshape(16, 128).T
        ppm[:, l, PP_GQ] = np.tile(inp["g_q"][l], 2)
        ppm[:, l, PP_GK] = np.tile(inp["g_k"][l], 2)
        ppm[:16, l, PP_BF] = inp["b_forget"][l]
    cst = np.zeros((128, 5, 128), np.float32)
    cst[:, 0] = np.eye(128)
    cst[:, 1] = 1.0
    cst[:64, 2, :64] = 1.0
    cst[64:, 2, 64:] = 1.0
    cst[:, 3] = np.eye(128)
    kk, qq = np.meshgrid(np.arange(128), np.arange(128), indexing="ij")
    cst[:, 4] = np.where(kk > qq, NEG, 0.0)
    return {
        "wup1": up_layout(inp["w_up1"]), "wup2": up_layout(inp["w_up2"]),
        "wdn1": dn_layout(inp["w_down1"]), "wdn2": dn_layout(inp["w_down2"]),
        "wlx": wlx, "wqkv": wqkv, "wf": wf, "wax": wax, "wmg": wmg, "wo": wo,
        "pp": f(ppm.reshape(128, NL * NPP)), "cst": f(cst.reshape(128, 640)),
    }


_CACHE = {}


def kernel(**inputs):
    inp = {k: np.asarray(v) for k, v in inputs.items()}
    x = np.ascontiguousarray(inp["x"], dtype=np.float32)
    shared = _prep_weights(inp)
    key = "full"
    if key not in _CACHE:
        _CACHE[key] = build_program()
    nc = _CACHE[key]
    n = 8
    in_maps = []
    for c in range(n):
        m = dict(shared)
        m["x"] = np.ascontiguousarray(x[2 * c:2 * c + 2])
        in_maps.append(m)
    res = run_bass_kernel_spmd(nc, in_maps, core_ids=list(range(n)))
    out = np.concatenate([np.asarray(r["y"], dtype=np.float32) for r in res.results], axis=0)
    return out
```
